# Optimizing a Trainium2 kernel written in Bass

```python
import math
import numpy as np
import jax
import jax.numpy as jnp
from jax import lax

D_MODEL = 2048
BATCH = 8
SEQ = 4096
DEPTH = 2
DEC_BATCH = 2
DEC_SEQ = 8192
PAST_LEN = 128

GRID_W = 64
EPS = 1e-6

NA_HEADS = 8
NA_HEAD_DIM = 128
NA_WIDTH = NA_HEADS * NA_HEAD_DIM
NA_KR_MAX = 8
NA_KC = 16
NA_QBLK = 16
NA_KBLK = 32

ML_HEADS = 4
ML_HEAD_DIM = 256
ML_WIDTH = ML_HEADS * ML_HEAD_DIM
ML_CHUNK = 64
ML_CONV_W = 3

SG_GROUPS = 8
SG_CHUNK = 128
SG_WIDTH = 1024
SG_GROUP_DIM = SG_WIDTH // SG_GROUPS

D_FF = 4 * D_MODEL
N_BRANCH = 3

OFF_A = 0
OFF_B = OFF_A + 3 * NA_WIDTH
OFF_BG = OFF_B + 4 * ML_WIDTH
OFF_C = OFF_BG + 4 * ML_HEADS
OFF_G = OFF_C + 2 * SG_WIDTH
D_IN = OFF_G + N_BRANCH * D_MODEL

kernel_name = "hybrid_bidir_natten_mlstm_sgmlp_encoder"


def rmsnorm(x, g):
    xf = x.astype(jnp.float32)
    y = xf * lax.rsqrt(jnp.mean(xf * xf, axis=-1, keepdims=True) + EPS)
    return (y * g.astype(jnp.float32)).astype(x.dtype)


def centred_dwconv(x, w):
    K = w.shape[0]
    p = K // 2
    T = x.shape[1]
    xp = jnp.pad(x, ((0, 0), (p, p), (0, 0)))
    return sum(xp[:, j:j + T] * w[j] for j in range(K))


def neighbourhood_attention(q, k, v, rpb):
    B, T, H, Dh = q.shape
    rows = T // GRID_W
    kr = min(NA_KR_MAX, rows)
    n_cb = GRID_W // NA_QBLK
    q_cols = np.arange(GRID_W).reshape(n_cb, NA_QBLK)
    kc0 = np.clip(q_cols[:, 0] - NA_KC // 2, 0, GRID_W - NA_KBLK)
    k_cols = kc0[:, None] + np.arange(NA_KBLK)[None, :]
    cs = np.clip(q_cols - NA_KC // 2, 0, GRID_W - NA_KC)
    col_ok = (k_cols[:, None, :] >= cs[:, :, None]) & (k_cols[:, None, :] < cs[:, :, None] + NA_KC)
    dc_idx = np.clip(k_cols[:, None, :] - q_cols[:, :, None] + NA_KC - 1, 0, 2 * NA_KC - 2)
    col_bias = rpb[:, :, dc_idx]
    mask = jnp.asarray(col_ok)[None, None, :, :, None, :]
    qg = q.reshape(B, rows, n_cb, NA_QBLK, H, Dh)
    kg = k.reshape(B, rows, GRID_W, H, Dh)
    vg = v.reshape(B, rows, GRID_W, H, Dh)
    scale = Dh ** -0.5

    def one_row(r):
        rs = jnp.clip(r - kr // 2, 0, rows - kr)
        q_r = lax.dynamic_index_in_dim(qg, r, axis=1, keepdims=False)
        k_r = lax.dynamic_slice_in_dim(kg, rs, kr, axis=1)[:, :, k_cols]
        v_r = lax.dynamic_slice_in_dim(vg, rs, kr, axis=1)[:, :, k_cols]
        bias = col_bias[:, rs - r + NA_KR_MAX - 1 + jnp.arange(kr)]
        bias = bias.transpose(0, 2, 3, 1, 4)[None].astype(jnp.float32)
        s = jnp.einsum('bnqhd,brnkhd->bhnqrk', q_r, k_r).astype(jnp.float32) * scale + bias
        s = jnp.where(mask, s, -jnp.inf)
        p = jax.nn.softmax(s.reshape(B, H, n_cb, NA_QBLK, kr * NA_KBLK), axis=-1)
        p = p.reshape(B, H, n_cb, NA_QBLK, kr, NA_KBLK).astype(v.dtype)
        return jnp.einsum('bhnqrk,brnkhd->bnqhd', p, v_r)

    out = lax.map(one_row, jnp.arange(rows))
    return jnp.moveaxis(out, 0, 1).reshape(B, T, H * Dh)


def mlstm_chunkwise(q, k, v, i_pre, f_pre):
    B, H, T, Dh = q.shape
    L = ML_CHUNK
    nc = T // L

    def to_chunks(a):
        a = a.reshape(B, H, nc, L, *a.shape[3:])
        return jnp.moveaxis(a, 2, 0)

    logf = jax.nn.log_sigmoid(f_pre)
    xs = (to_chunks(q), to_chunks(k), to_chunks(v), to_chunks(i_pre), to_chunks(logf))
    lower = jnp.tril(jnp.ones((L, L), dtype=bool))

    def step(carry, inp):
        C, n, m = carry
        qc, kc, vc, ic, lfc = inp
        b = jnp.cumsum(lfc, axis=-1)
        dlog = jnp.where(lower, b[..., :, None] - b[..., None, :] + ic[..., None, :], -jnp.inf)
        inter = b + m[..., None]
        m_row = jnp.maximum(inter, jnp.max(dlog, axis=-1))
        s = jnp.einsum('bhqd,bhkd->bhqk', qc, kc) * jnp.exp(dlog - m_row[..., None])
        w_inter = jnp.exp(inter - m_row)
        num = jnp.einsum('bhqk,bhkd->bhqd', s, vc) + w_inter[..., None] * jnp.einsum('bhqd,bhde->bhqe', qc, C)
        nq = jnp.sum(s, axis=-1) + w_inter * jnp.einsum('bhqd,bhd->bhq', qc, n)
        h = num / jnp.maximum(jnp.abs(nq), jnp.exp(-m_row))[..., None]
        b_last = b[..., -1]
        wlog = b_last[..., None] - b + ic
        m_new = jnp.maximum(b_last + m, jnp.max(wlog, axis=-1))
        w = jnp.exp(wlog - m_new[..., None])
        decay = jnp.exp(b_last + m - m_new)
        C_new = decay[..., None, None] * C + jnp.einsum('bhs,bhsd,bhse->bhde', w, kc, vc)
        n_new = decay[..., None] * n + jnp.einsum('bhs,bhsd->bhd', w, kc)
        return (C_new, n_new, m_new), h

    init = (jnp.zeros((B, H, Dh, Dh), jnp.float32), jnp.zeros((B, H, Dh), jnp.float32),
            jnp.zeros((B, H), jnp.float32))
    _, hs = lax.scan(step, init, xs)
    return jnp.moveaxis(hs, 0, 2).reshape(B, H, T, Dh)


def mlstm_branch(qkvo, gate_pre, conv_w, norm_g):
    B, T, _ = qkvo.shape
    qk = jax.nn.silu(centred_dwconv(qkvo[..., :2 * ML_WIDTH], conv_w))
    v = qkvo[..., 2 * ML_WIDTH:3 * ML_WIDTH]
    o = qkvo[..., 3 * ML_WIDTH:]

    def heads(a):
        return a.reshape(B, T, ML_HEADS, ML_HEAD_DIM).transpose(0, 2, 1, 3).astype(jnp.float32)

    qh = heads(qk[..., :ML_WIDTH]) * (ML_HEAD_DIM ** -0.5)
    kh = heads(qk[..., ML_WIDTH:])
    vh = heads(v)
    g = gate_pre.astype(jnp.float32).reshape(B, T, 4, ML_HEADS).transpose(2, 0, 3, 1)
    h_fwd = mlstm_chunkwise(qh, kh, vh, g[0], g[1])
    flip = lambda a: jnp.flip(a, axis=2)
    h_bwd = flip(mlstm_chunkwise(flip(qh), flip(kh), flip(vh), flip(g[2]), flip(g[3])))
    h = h_fwd + h_bwd
    h = h * lax.rsqrt(jnp.mean(h * h, axis=-1, keepdims=True) + EPS)
    h = h.transpose(0, 2, 1, 3).reshape(B, T, ML_WIDTH) * norm_g.astype(jnp.float32)
    return (h * jax.nn.sigmoid(o.astype(jnp.float32))).astype(qkvo.dtype)


def spatial_gating_branch(uv, norm_g, w_s, b_s):
    B, T, _ = uv.shape
    z = jax.nn.gelu(uv)
    u = z[..., :SG_WIDTH]
    v = rmsnorm(z[..., SG_WIDTH:], norm_g)
    vg = v.reshape(B, T // SG_CHUNK, SG_CHUNK, SG_GROUPS, SG_GROUP_DIM)
    mixed = jnp.einsum('gpq,bcqgd->bcpgd', w_s, vg) + b_s.T[None, None, :, :, None]
    return u * mixed.reshape(B, T, SG_WIDTH)


def trunk_layer(x, pre_mix_g, post_mix_g, pre_mlp_g, post_mlp_g, w_in, b_gate, conv_w, na_rpb,
                ml_norm_g, sg_norm_g, w_s, b_s, w_a, w_b, w_c, w_out, w_up, w_down):
    B, T, _ = x.shape
    h = rmsnorm(x, pre_mix_g)
    proj = lambda lo, hi: h @ w_in[:, lo:hi]
    qkv_a = proj(OFF_A, OFF_B).reshape(B, T, 3, NA_HEADS, NA_HEAD_DIM)
    y_a = neighbourhood_attention(qkv_a[:, :, 0], qkv_a[:, :, 1], qkv_a[:, :, 2], na_rpb)
    y_b = mlstm_branch(proj(OFF_B, OFF_BG), proj(OFF_BG, OFF_C) + b_gate, conv_w, ml_norm_g)
    y_c = spatial_gating_branch(proj(OFF_C, OFF_G), sg_norm_g, w_s, b_s)
    gates = jax.nn.sigmoid(proj(OFF_G, D_IN).reshape(B, T, N_BRANCH, D_MODEL))
    merged = gates[:, :, 0] * (y_a @ w_a) + gates[:, :, 1] * (y_b @ w_b) + gates[:, :, 2] * (y_c @ w_c)
    x = x + rmsnorm(merged @ w_out, post_mix_g)
    h = rmsnorm(x, pre_mlp_g)
    f = jnp.square(jax.nn.relu(h @ w_up)) @ w_down
    return x + rmsnorm(f, post_mlp_g)


def setup_inputs(seed: int = 0) -> dict:
    key = jax.random.key(seed)
    ks = jax.random.split(key, 24)
    L = DEPTH

    def nrm(k, shape, scale):
        return jax.random.normal(k, shape, jnp.float32) * scale

    i_bias = nrm(ks[8], (L, 2, ML_HEADS), 0.1)
    f_bias = jnp.linspace(3.0, 6.0, ML_HEADS, dtype=jnp.float32)[None, None, :] + nrm(ks[9], (L, 2, ML_HEADS), 0.1)
    b_gate = jnp.stack([i_bias[:, 0], f_bias[:, 0], i_bias[:, 1], f_bias[:, 1]], axis=1).reshape(L, 4 * ML_HEADS)
    return {
        "x_prompt": nrm(ks[0], (BATCH, SEQ, D_MODEL), 1.0),
        "x_sample": nrm(ks[1], (DEC_BATCH, DEC_SEQ, D_MODEL), 1.0),
        "pre_mix_g": 1.0 + nrm(ks[2], (L, D_MODEL), 0.1),
        "post_mix_g": 1.0 + nrm(ks[3], (L, D_MODEL), 0.1),
        "pre_mlp_g": 1.0 + nrm(ks[4], (L, D_MODEL), 0.1),
        "post_mlp_g": 1.0 + nrm(ks[5], (L, D_MODEL), 0.1),
        "w_in": nrm(ks[6], (L, D_MODEL, D_IN), D_MODEL ** -0.5),
        "b_gate": b_gate,
        "conv_w": nrm(ks[7], (L, ML_CONV_W, 2 * ML_WIDTH), ML_CONV_W ** -0.5),
        "na_rpb": nrm(ks[10], (L, NA_HEADS, 2 * NA_KR_MAX - 1, 2 * NA_KC - 1), 0.1),
        "ml_norm_g": 1.0 + nrm(ks[11], (L, ML_WIDTH), 0.1),
        "sg_norm_g": 1.0 + nrm(ks[12], (L, SG_WIDTH), 0.1),
        "w_s": nrm(ks[13], (L, SG_GROUPS, SG_CHUNK, SG_CHUNK), SG_CHUNK ** -0.5),
        "b_s": 1.0 + nrm(ks[14], (L, SG_GROUPS, SG_CHUNK), 0.1),
        "w_a": nrm(ks[15], (L, NA_WIDTH, D_MODEL), NA_WIDTH ** -0.5),
        "w_b": nrm(ks[16], (L, ML_WIDTH, D_MODEL), ML_WIDTH ** -0.5),
        "w_c": nrm(ks[17], (L, SG_WIDTH, D_MODEL), SG_WIDTH ** -0.5),
        "w_out": nrm(ks[18], (L, D_MODEL, D_MODEL), D_MODEL ** -0.5),
        "w_up": nrm(ks[19], (L, D_MODEL, D_FF), D_MODEL ** -0.5),
        "w_down": nrm(ks[20], (L, D_FF, D_MODEL), D_FF ** -0.5),
    }


def reference(x_prompt, x_sample, pre_mix_g, post_mix_g, pre_mlp_g, post_mlp_g, w_in, b_gate, conv_w,
              na_rpb, ml_norm_g, sg_norm_g, w_s, b_s, w_a, w_b, w_c, w_out, w_up, w_down):
    def run(x):
        for l in range(DEPTH):
            x = trunk_layer(x, pre_mix_g[l], post_mix_g[l], pre_mlp_g[l], post_mlp_g[l], w_in[l], b_gate[l],
                            conv_w[l], na_rpb[l], ml_norm_g[l], sg_norm_g[l], w_s[l], b_s[l], w_a[l], w_b[l],
                            w_c[l], w_out[l], w_up[l], w_down[l])
        return x

    y_prompt = run(x_prompt)
    y_sample = run(x_sample)
    return (y_prompt, y_sample)
```

```python
import numpy as np
from contextlib import ExitStack
import concourse.bass as bass
import concourse.mybir as mybir
from concourse.bass_utils import run_bass_kernel_spmd

F32 = mybir.dt.float32
BF16 = mybir.dt.bfloat16
AF = mybir.ActivationFunctionType
ALU = mybir.AluOpType
AX = mybir.AxisListType

D = 2048
DIN = 15376
DFF = 8192
EPS = 1e-6
OFF_A, OFF_B, OFF_BG, OFF_C, OFF_G = 0, 3072, 7168, 7184, 9232
NEG = -30000.0
ENGS = ("pe", "act", "dve", "pool", "sp")
NSLOT = 8


class Res:
    __slots__ = ("name", "writer", "readers")

    def __init__(self, name=""):
        self.name = name
        self.writer = None
        self.readers = {}


class Op:
    __slots__ = ("eng", "fn", "waits", "signal", "dma", "slot")


class Prog:
    def __init__(self):
        self.ops = {e: [] for e in ENGS}
        self.know = {e: {f: 0 for f in ENGS} for e in ENGS}
        self.known_dma = {e: set() for e in ENGS}
        self.dmas = []
        self.slot_last = {}
        self.slot_uses = {}
        self.qcount = {"sp": 0, "pool": 0}
        self.ambient = ()

    def fence(self, engs, res):
        deps = set()
        if res.writer is not None:
            deps.add(res.writer)
        deps.update(res.readers.values())
        amb = self.ambient
        self.ambient = ()
        for e in engs:
            self.op(e, None, extra_deps=deps)
        self.ambient = amb
        res.writer = None
        res.readers = {}

    def op(self, eng, fn, reads=(), writes=(), dma=False, extra_deps=()):
        reads = tuple(reads) + self.ambient
        deps = set(extra_deps)
        for r in reads:
            if r.writer is not None:
                deps.add(r.writer)
        for w in writes:
            if w.writer is not None:
                deps.add(w.writer)
            deps.update(w.readers.values())
        o = Op()
        o.eng, o.fn, o.signal, o.dma, o.waits, o.slot = eng, fn, False, dma, [], None
        idx = len(self.ops[eng]) + 1
        if dma:
            s = self.qcount[eng] % NSLOT
            self.qcount[eng] += 1
            prev = self.slot_last.get((eng, s))
            if prev is not None:
                deps.add(("d", prev))
            uses = self.slot_uses.get((eng, s), 0) + 1
            self.slot_uses[(eng, s)] = uses
            did = len(self.dmas)
            self.dmas.append((eng, s, uses * 16))
            self.slot_last[(eng, s)] = did
            o.slot = (eng, s)
            tok = ("d", did)
        else:
            tok = ("c", eng, idx)
        cdeps = {}
        for d in deps:
            if d[0] == "c":
                _, e2, i2 = d
                if e2 == eng and eng == "pe":
                    continue
                if self.know[eng][e2] >= i2:
                    continue
                cdeps[e2] = max(cdeps.get(e2, 0), i2)
            else:
                if d[1] in self.known_dma[eng]:
                    continue
                self.known_dma[eng].add(d[1])
                o.waits.append(d)
        for e2, i2 in cdeps.items():
            self.know[eng][e2] = i2
            self.ops[e2][i2 - 1].signal = True
            o.waits.append(("c", e2, i2))
        self.ops[eng].append(o)
        for w in writes:
            w.writer = tok
            w.readers = {}
        for r in reads:
            if r in writes:
                continue
            key = tok[1] if tok[0] == "c" else ("d",) + o.slot
            r.readers[key] = tok
        return tok

    def barrier(self):
        last = {}
        for e in ENGS:
            for i in range(len(self.ops[e]), 0, -1):
                oo = self.ops[e][i - 1]
                if (not oo.dma) and oo.fn is not None:
                    last[e] = i
                    break
        dl = list(self.slot_last.values())
        for e in ENGS:
            o = Op()
            o.eng, o.fn, o.signal, o.dma, o.waits, o.slot = e, None, False, False, [], None
            for e2, idx in last.items():
                if e2 == e and e == "pe":
                    continue
                if self.know[e][e2] >= idx:
                    continue
                self.know[e][e2] = idx
                self.ops[e2][idx - 1].signal = True
                o.waits.append(("c", e2, idx))
            for did in dl:
                if did not in self.known_dma[e]:
                    self.known_dma[e].add(did)
                    o.waits.append(("d", did))
            self.ops[e].append(o)

    def emit(self, nc, stack):
        sems = {e: stack.enter_context(nc.semaphore("s_" + e)) for e in ENGS}
        dsem = {}
        for q in ("sp", "pool"):
            for s in range(NSLOT):
                dsem[(q, s)] = stack.enter_context(nc.semaphore("d_%s_%d" % (q, s)))
        sigc = {}
        for e in ENGS:
            c = 0
            arr = []
            for o in self.ops[e]:
                if o.signal and not o.dma and o.fn is not None:
                    c += 1
                arr.append(c)
            sigc[e] = arr
        block = stack.enter_context(nc.Block())

        def body(e):
            def f(h):
                for o in self.ops[e]:
                    for w in o.waits:
                        if w[0] == "c":
                            h.wait_ge(sems[w[1]], sigc[w[1]][w[2] - 1])
                        else:
                            q, s, cnt = self.dmas[w[1]]
                            h.wait_ge(dsem[(q, s)], cnt)
                    if o.fn is None:
                        continue
                    ins = o.fn(h)
                    if o.dma:
                        ins.then_inc(dsem[o.slot], 16)
                    elif o.signal:
                        ins.then_inc(sems[e], 1)
            return f

        block.tensor(body("pe"))
        block.scalar(body("act"))
        block.vector(body("dve"))
        block.gpsimd(body("pool"))
        block.sync(body("sp"))


class Arena:
    def __init__(self, tensor, nelem):
        self.t = tensor
        self.n = nelem
        self.off = 0

    def reset(self):
        self.off = 0

    def _alloc(self, nbytes):
        nb = (nbytes + 63) // 64 * 64
        o = self.off
        self.off += nb // 2
        assert self.off <= self.n, "arena overflow %d > %d" % (self.off * 2, self.n * 2)
        return o

    def bf(self, n, parts=128):
        o = self._alloc(n * 2)
        return self.t[0:parts, o:o + n]

    def f32(self, n, parts=128):
        o = self._alloc(n * 4)
        return self.t[0:parts, o:o + 2 * n].bitcast(F32)


def build(seqs, depth=2, debug=False, vsplit=None):
    T = sum(seqs)
    NT = min(1024, min(seqs))
    NTC = 512
    assert all(s % NT == 0 and s % 512 == 0 for s in seqs)
    nc = bass.Bass("TRN2", target_bir_lowering=False)
    kind_s = "ExternalOutput" if debug else "Internal"

    def din(name, shape, dt=F32):
        return nc.dram_tensor(name, list(shape), dt, kind="ExternalInput").ap()

    def dscr(name, shape, dt, dbg=False):
        return nc.dram_tensor(name, list(shape), dt, kind=(kind_s if dbg else "Internal")).ap()

    x_in = din("x", [T, D])
    w_in = din("w_in", [depth, D, DIN])
    w_a = din("w_a", [depth, 1024, D])
    w_b = din("w_b", [depth, 1024, D])
    w_c = din("w_c", [depth, 1024, D])
    w_out = din("w_out", [depth, D, D])
    w_up = din("w_up", [depth, D, DFF])
    w_down = din("w_down", [depth, DFF, D])
    gvec = din("gvec", [depth, 4, 128, 16])
    bgate = din("bgate", [depth, 16, 1])
    convw = din("convw", [depth, 128, 48])
    mlg = din("mlg", [depth, 1, 1024])
    sgg = din("sgg", [depth, 1, 1024])
    wsT = din("wsT", [depth, 128, 1024])
    bsv = din("bsv", [depth, 1, 1024])
    Utab = din("Utab", [depth, 128, 8 * 14 * 64])
    ident_d = din("ident", [128, 128])
    flag_d = din("flag", [128, 2])
    seq_starts = [sum(seqs[:i]) for i in range(len(seqs))]

    def is_vb(tok):
        return vsplit is not None and any(tok == st + vsplit for st in seq_starts)
    trif_d = din("trif", [128, 128])
    trib_d = din("trib", [128, 128])
    y_out = nc.dram_tensor("y", [T, D], F32, kind="ExternalOutput").ap()

    win_bf = dscr("win_bf", [depth, D, DIN], BF16)
    wa_bf = dscr("wa_bf", [depth, 1024, D], BF16)
    wb_bf = dscr("wb_bf", [depth, 1024, D], BF16)
    wc_bf = dscr("wc_bf", [depth, 1024, D], BF16)
    wout_bf = dscr("wout_bf", [depth, D, D], BF16)
    wup_bf = dscr("wup_bf", [depth, D, DFF], BF16)
    wdown_bf = dscr("wdown_bf", [depth, DFF, D], BF16)
    na_qT = dscr("na_qT", [8, 128, T], BF16, True)
    na_kT = dscr("na_kT", [8, 128, T], BF16, True)
    na_v = dscr("na_v", [T, 1024], BF16, True)
    ml_qT = dscr("ml_qT", [8, 128, T], BF16, True)
    ml_kT = dscr("ml_kT", [8, 128, T], BF16, True)
    ml_v = dscr("ml_v", [T, 1024], BF16, True)
    ml_o = dscr("ml_o", [T, 1024], F32, True)
    ml_g = dscr("ml_g", [16, T], F32, True)
    sg_uT = dscr("sg_uT", [8, 128, T], BF16, True)
    sg_v = dscr("sg_v", [T, 1024], F32, True)
    yaT = dscr("yaT", [8, 128, T], BF16, True)
    ybT = dscr("ybT", [8, 128, T], BF16, True)
    ycT = dscr("ycT", [8, 128, T], BF16, True)
    hfwd = dscr("hfwd", [T, 1024], F32, True)
    xmid = dscr("xmid", [T, D], F32, True)

    stack = ExitStack()
    ARN = 105472
    arena_t = stack.enter_context(nc.sbuf_tensor("arena", [128, ARN], BF16))
    A = Arena(arena_t, ARN)
    PS = [stack.enter_context(nc.psum_tensor("ps%d" % i, [128, 512], F32)) for i in range(8)]
    P = Prog()
    PSR = [Res("ps%d" % i) for i in range(8)]

    def dma(q, out, in_, reads=(), writes=(), **kw):
        P.op(q, lambda h: h.dma_start(out=out, in_=in_, **kw), reads, writes, dma=True)

    def pe_group(mms, reads, writes):
        def fn(h):
            ins = None
            for (out, lhsT, rhs, st, sp_) in mms:
                ins = h.matmul(out, lhsT=lhsT, rhs=rhs, start=st, stop=sp_)
            return ins
        P.op("pe", fn, reads, writes)

    def pe_tr(trs, ident, reads, writes):
        def fn(h):
            ins = None
            for (out, in_) in trs:
                ins = h.transpose(out=out, in_=in_, identity=ident)
            return ins
        P.op("pe", fn, reads, writes)

    def act(out, in_, func, reads, writes, **kw):
        P.op("act", lambda h: h.activation(out=out, in_=in_, func=func, **kw), reads, writes)

    def dve_tt(out, in0, in1, op, reads, writes, eng="dve"):
        P.op(eng, lambda h: h.tensor_tensor(out=out, in0=in0, in1=in1, op=op), reads, writes)

    def dve_ts(out, in0, s1, s2, op0, op1, reads, writes, eng="dve"):
        if op1 is None:
            P.op(eng, lambda h: h.tensor_scalar(out=out, in0=in0, scalar1=s1, scalar2=None, op0=op0), reads, writes)
        else:
            P.op(eng, lambda h: h.tensor_scalar(out=out, in0=in0, scalar1=s1, scalar2=s2, op0=op0, op1=op1), reads, writes)

    def dve_stt(out, in0, scalar, in1, op0, op1, reads, writes):
        P.op("dve", lambda h: h.scalar_tensor_tensor(out=out, in0=in0, scalar=scalar, in1=in1, op0=op0, op1=op1), reads, writes)

    def dve_recip(out, in_, reads, writes):
        P.op("dve", lambda h: h.reciprocal(out=out, in_=in_), reads, writes)

    def memset(ap, val, writes, eng="dve"):
        P.op(eng, lambda h: h.memset(ap, val), (), writes)

    evac_rr = [0]

    def evac_copy(out, in_, reads, writes, scale=None):
        evac_rr[0] ^= 1
        if evac_rr[0]:
            if scale is None:
                act(out, in_, AF.Copy, reads, writes)
            else:
                act(out, in_, AF.Copy, reads, writes, scale=float(scale))
        else:
            if scale is None:
                P.op("dve", lambda h: h.tensor_copy(out=out, in_=in_), reads, writes)
            else:
                dve_ts(out, in_, float(scale), None, ALU.mult, None, reads, writes)

    def cast_list(l):
        out = []

        def add(dst, src, rows, rb):
            for r0 in range(0, rows, rb):
                out.append(lambda r0=r0: dma("pool", dst[l, r0:r0 + rb, :], src[l, r0:r0 + rb, :], max_dma_last_dim=4096))
        add(win_bf, w_in, D, 128)
        add(wa_bf, w_a, 1024, 256)
        add(wb_bf, w_b, 1024, 256)
        add(wc_bf, w_c, 1024, 256)
        add(wout_bf, w_out, D, 256)
        add(wup_bf, w_up, D, 128)
        add(wdown_bf, w_down, DFF, 512)
        return out

    bg_casts = []

    def phase_cast():
        for f in cast_list(0):
            f()

    class NormCtx:
        pass

    def make_normctx(identv):
        c = NormCtx()
        c.ident = identv
        c.xb = [A.f32(2048), A.f32(2048)]
        c.xbR = [Res("xb0"), Res("xb1")]
        c.xn = A.f32(2048)
        c.xnR = Res("xn")
        c.junk = A.bf(2048)
        c.junkR = Res("junk")
        c.st = A.f32(8)
        c.stR = Res("st")
        c.cnt = 0
        return c

    TRB = (0, 1, 6, 7)
    trcnt = [0]

    def trbank():
        b = TRB[trcnt[0] % len(TRB)]
        trcnt[0] += 1
        return b

    def norm_tile(c, xsrc_rows, gT, gR, hTv, hR, col, rawT=None, rawR=None, constR=(), defer=None):
        i = c.cnt % 2
        c.cnt += 1
        xb, xbR = c.xb[i], c.xbR[i]

        def head():
            dma("sp", xb, xsrc_rows, (), (xbR,))
            act(c.junk, xb, AF.Square, (xbR,), (c.junkR, c.stR), accum_out=c.st[:, 0:1])
            act(c.st[:, 1:2], c.st[:, 0:1], AF.Sqrt, (c.stR,), (c.stR,), scale=1.0 / D, bias=c.eps[:, 0:1])
            dve_recip(c.st[:, 2:3], c.st[:, 1:2], (c.stR,), (c.stR,))
            dve_ts(c.xn, xb, c.st[:, 2:3], None, ALU.mult, None, (xbR, c.stR), (c.xnR,))

        def blk(b):
            pb = trbank()
            pe_tr([(PS[pb][:, j * 128:(j + 1) * 128], c.xn[:, (4 * b + j) * 128:(4 * b + j + 1) * 128]) for j in range(4)],
                  c.ident, (c.xnR,) + tuple(constR), (PSR[pb],))
            for j in range(4):
                kc = 4 * b + j
                o = hTv[:, kc, col:col + 128]
                src = PS[pb][:, j * 128:(j + 1) * 128]
                if j % 2 == 0:
                    act(o, src, AF.Copy, (PSR[pb], gR), (hR,), scale=gT[:, kc:kc + 1])
                else:
                    dve_ts(o, src, gT[:, kc:kc + 1], None, ALU.mult, None, (PSR[pb], gR), (hR,))

        def rawblk(b):
            pb = trbank()
            pe_tr([(PS[pb][:, j * 128:(j + 1) * 128], xb[:, (4 * b + j) * 128:(4 * b + j + 1) * 128]) for j in range(4)],
                  c.ident, (xbR,) + tuple(constR), (PSR[pb],))
            evac_copy(rawT[:, 4 * b:4 * b + 4, col:col + 128],
                      PS[pb][:, :].rearrange("p (a b) -> p a b", a=4), (PSR[pb],), (rawR,))

        thunks = [head] + [(lambda b=b: blk(b)) for b in range(4)]
        if rawT is not None:
            thunks += [(lambda b=b: rawblk(b)) for b in range(4)]
        if defer is None:
            for t_ in thunks:
                t_()
        else:
            defer.extend(thunks)

    def load_const(q, view, src, R):
        dma(q, view, src, (), (R,))

    def phase_A(l, xsrc):
        P.barrier()
        A.reset()
        cR = Res("constA")
        ident = A.f32(128)
        load_const("sp", ident, ident_d[:, :], cR)
        gT = A.f32(16)
        load_const("sp", gT, gvec[l, 0], cR)
        cw = A.f32(48)
        load_const("sp", cw, convw[l], cR)
        bg = A.f32(1, parts=16)
        load_const("sp", bg, bgate[l], cR)
        flg = A.f32(2)
        load_const("sp", flg, flag_d[:, :], cR)
        nctx = make_normctx(ident)
        nctx.eps = A.f32(1)
        memset(nctx.eps, EPS, (cR,))
        W = NT + 2
        hT = [A.bf(16 * W).rearrange("p (k n) -> p k n", k=16) for _ in range(2)]
        hR = [Res("hT0"), Res("hT1")]
        wp = [A.bf(16 * 512).rearrange("p (k n) -> p k n", k=16) for _ in range(2)]
        wR = [Res("wp0"), Res("wp1")]
        sfm = [A.bf(NT) for _ in range(2)]
        sfmR = [Res("sfm0"), Res("sfm1")]
        pre = [A.f32(W) for _ in range(2)]
        preR = [Res("pre0"), Res("pre1")]
        acc = [A.f32(NT) for _ in range(2)]
        accR = [Res("acc0"), Res("acc1")]
        stm = [A.bf(512) for _ in range(2)]
        stmR = [Res("stm0"), Res("stm1")]
        stf = [A.f32(512) for _ in range(2)]
        stfR = [Res("stf0"), Res("stf1")]
        gst = A.f32(NT, parts=16)
        gstR = Res("gst")
        winl = win_bf[l].rearrange("(k p) c -> p k c", p=128)
        nsl = (W + 511) // 512
        wsl = W // nsl
        assert wsl * nsl == W
        tiles = []
        s0 = 0
        for sl in seqs:
            for m in range(sl // NT):
                tiles.append((s0 + m * NT, m == 0, m == sl // NT - 1))
            s0 += sl
        bank = [2]
        cnt = {"wp": 0, "sfm": 0, "pre": 0, "stm": 0, "stf": 0}

        def nb():
            b = bank[0]
            bank[0] = 2 + (bank[0] - 2 + 1) % 4
            tick()
            return b

        deferred = []

        def tick(n=1):
            for _ in range(n):
                if deferred:
                    deferred.pop(0)()

        def do_norm(mi, defer=False):
            t0, first, last = tiles[mi]
            hv, hr = hT[mi % 2], hR[mi % 2]
            for tt in range(NT // 128):
                norm_tile(nctx, xsrc[t0 + tt * 128:t0 + (tt + 1) * 128, :], gT, cR, hv, hr, 1 + tt * 128, constR=(cR,),
                          defer=(deferred if defer else None))

            def halo():
                if first:
                    memset(hv[:, :, 0:1], 0.0, (hr,))
                else:
                    pv, pr = hT[(mi - 1) % 2], hR[(mi - 1) % 2]
                    if is_vb(t0):
                        dve_ts(hv[:, :, 0:1], pv[:, :, NT:NT + 1], flg[:, 0:1], None, ALU.mult, None, (pr, cR), (hr,))
                    else:
                        P.op("dve", lambda h: h.tensor_copy(out=hv[:, :, 0:1], in_=pv[:, :, NT:NT + 1]), (pr,), (hr,))
            if defer:
                deferred.append(halo)
            else:
                halo()

        def load_panel(c0, ncols):
            i = cnt["wp"] % 2
            cnt["wp"] += 1
            dma("sp", wp[i][:, :, 0:ncols], winl[:, :, c0:c0 + ncols], (), (wR[i],))
            return wp[i], wR[i]

        def proj(mi):
            t0, first, last = tiles[mi]
            hv, hr = hT[mi % 2], hR[mi % 2]
            def right_halo():
                tick(len(deferred))
                if last:
                    memset(hv[:, :, NT + 1:NT + 2], 0.0, (hr,))
                else:
                    nv, nr = hT[(mi + 1) % 2], hR[(mi + 1) % 2]
                    if is_vb(t0 + NT):
                        dve_ts(hv[:, :, NT + 1:NT + 2], nv[:, :, 1:2], flg[:, 0:1], None, ALU.mult, None, (nr, cR), (hr,))
                    else:
                        P.op("dve", lambda h: h.tensor_copy(out=hv[:, :, NT + 1:NT + 2], in_=nv[:, :, 1:2]), (nr,), (hr,))

            def fm_plain(c0, dst, func, scale):
                wv, wr = load_panel(c0, 512)
                for j in range(4):
                    i = cnt["sfm"] % 2
                    cnt["sfm"] += 1
                    for ts in range(NT // 512):
                        b = nb()
                        pe_group([(PS[b][:, :], wv[:, kc, j * 128:(j + 1) * 128], hv[:, kc, 1 + ts * 512:1 + (ts + 1) * 512],
                                   kc == 0, kc == 15) for kc in range(16)], (wr, hr), (PSR[b],))
                        o = sfm[i][:, ts * 512:(ts + 1) * 512]
                        if func is None:
                            evac_copy(o, PS[b][:, :], (PSR[b],), (sfmR[i],), scale=scale)
                        else:
                            act(o, PS[b][:, :], func, (PSR[b],), (sfmR[i],))
                    dma("pool", dst(j)[:, t0:t0 + NT], sfm[i], (sfmR[i],), ())

            def fm_conv(c0, cc0, dst, qscale):
                wv, wr = load_panel(c0, 512)
                for j in range(4):
                    cc = cc0 + j
                    ip = cnt["pre"] % 2
                    cnt["pre"] += 1
                    for s in range(nsl):
                        b = nb()
                        pe_group([(PS[b][:, 0:wsl], wv[:, kc, j * 128:(j + 1) * 128], hv[:, kc, s * wsl:(s + 1) * wsl],
                                   kc == 0, kc == 15) for kc in range(16)], (wr, hr), (PSR[b],))
                        evac_copy(pre[ip][:, s * wsl:(s + 1) * wsl], PS[b][:, 0:wsl], (PSR[b],), (preR[ip],))
                    pv = pre[ip]
                    av = acc[ip]
                    dve_ts(av, pv[:, 0:NT], cw[:, cc * 3:cc * 3 + 1], None, ALU.mult, None, (preR[ip], cR), (accR[ip],))
                    dve_stt(av, pv[:, 1:NT + 1], cw[:, cc * 3 + 1:cc * 3 + 2], av, ALU.mult, ALU.add, (preR[ip], cR), (accR[ip],))
                    dve_stt(av, pv[:, 2:NT + 2], cw[:, cc * 3 + 2:cc * 3 + 3], av, ALU.mult, ALU.add, (preR[ip], cR), (accR[ip],))
                    i = cnt["sfm"] % 2
                    cnt["sfm"] += 1
                    if qscale is None:
                        act(sfm[i], av, AF.Silu, (accR[ip],), (sfmR[i],))
                    else:
                        act(av, av, AF.Silu, (accR[ip],), (accR[ip],))
                        dve_ts(sfm[i], av, float(qscale), None, ALU.mult, None, (accR[ip],), (sfmR[i],))
                    dma("pool", dst(j)[:, t0:t0 + NT], sfm[i], (sfmR[i],), ())

            def tm_panel(c0, dst, dcol, f32out, func):
                wv, wr = load_panel(c0, 512)
                for tt in range(NT // 128):
                    b = nb()
                    pe_group([(PS[b][:, :], hv[:, kc, 1 + tt * 128:1 + (tt + 1) * 128], wv[:, kc, :],
                               kc == 0, kc == 15) for kc in range(16)], (wr, hr), (PSR[b],))
                    if f32out:
                        i = cnt["stf"] % 2
                        cnt["stf"] += 1
                        sv, sr = stf[i], stfR[i]
                    else:
                        i = cnt["stm"] % 2
                        cnt["stm"] += 1
                        sv, sr = stm[i], stmR[i]
                    if func is None:
                        evac_copy(sv, PS[b][:, :], (PSR[b],), (sr,))
                    else:
                        act(sv, PS[b][:, :], func, (PSR[b],), (sr,))
                    dma("pool", dst[t0 + tt * 128:t0 + (tt + 1) * 128, dcol:dcol + 512], sv, (sr,), ())

            for pi in range(2):
                fm_plain(OFF_A + pi * 512, lambda j, pi=pi: na_qT[pi * 4 + j], None, 128.0 ** -0.5)
            for pi in range(2):
                fm_plain(OFF_A + 1024 + pi * 512, lambda j, pi=pi: na_kT[pi * 4 + j], None, None)
            for pi in range(2):
                tm_panel(OFF_A + 2048 + pi * 512, na_v, pi * 512, False, None)
            right_halo()
            for pi in range(2):
                fm_conv(OFF_B + pi * 512, pi * 4, lambda j, pi=pi: ml_qT[pi * 4 + j], 1.0 / 16.0)
            for pi in range(2):
                fm_conv(OFF_B + 1024 + pi * 512, 8 + pi * 4, lambda j, pi=pi: ml_kT[pi * 4 + j], None)
            for pi in range(2):
                tm_panel(OFF_B + 2048 + pi * 512, ml_v, pi * 512, False, None)
            for pi in range(2):
                tm_panel(OFF_B + 3072 + pi * 512, ml_o, pi * 512, True, None)
            wv, wr = load_panel(OFF_BG, 16)
            for ts in range(NT // 512):
                b = nb()
                pe_group([(PS[b][0:16, :], wv[:, kc, 0:16], hv[:, kc, 1 + ts * 512:1 + (ts + 1) * 512], kc == 0, kc == 15)
                          for kc in range(16)], (wr, hr), (PSR[b],))
                act(gst[:, ts * 512:(ts + 1) * 512], PS[b][0:16, :], AF.Identity, (PSR[b], cR), (gstR,), bias=bg[:, 0:1])
            dma("pool", ml_g[:, t0:t0 + NT], gst, (gstR,), ())
            for pi in range(2):
                fm_plain(OFF_C + pi * 512, lambda j, pi=pi: sg_uT[pi * 4 + j], AF.Gelu_apprx_tanh, None)
            for pi in range(2):
                tm_panel(OFF_C + 1024 + pi * 512, sg_v, pi * 512, True, AF.Gelu_apprx_tanh)

        do_norm(0)
        for mi in range(len(tiles)):
            if mi + 1 < len(tiles) and not tiles[mi][2]:
                do_norm(mi + 1, defer=True)
            proj(mi)
            tick(len(deferred))
            if mi + 1 < len(tiles) and tiles[mi][2]:
                do_norm(mi + 1)

    def phase_SG(l):
        P.barrier()
        A.reset()
        cR = Res("constSG")
        sggb = A.f32(1024)
        load_const("sp", sggb, sgg[l].partition_broadcast(128), cR)
        bsb = A.f32(1024)
        load_const("sp", bsb, bsv[l].partition_broadcast(128), cR)
        wsf = A.f32(1024)
        load_const("sp", wsf, wsT[l], cR)
        wsb = A.bf(1024)
        P.op("dve", lambda h: h.tensor_copy(out=wsb, in_=wsf), (cR,), (cR,))
        eps = A.f32(1)
        memset(eps, EPS, (cR,))
        uT = [A.bf(8 * 512).rearrange("p (g t) -> p g t", g=8) for _ in range(2)]
        uR = [Res("u0"), Res("u1")]
        vf = [A.f32(1024) for _ in range(2)]
        vR = [Res("v0"), Res("v1")]
        vn = [A.bf(1024) for _ in range(2)]
        vnR = [Res("vn0"), Res("vn1")]
        junk = A.bf(1024)
        junkR = Res("junk")
        st = A.f32(8)
        stR = Res("st")
        tmp = [A.f32(512) for _ in range(2)]
        tmpR = [Res("t0"), Res("t1")]
        yst = [A.bf(8 * 512).rearrange("p (g t) -> p g t", g=8) for _ in range(2)]
        yR = [Res("y0"), Res("y1")]
        k = 0
        for t0 in range(0, T, 512):
            bi = (t0 // 512) % 2
            dma("sp", uT[bi], sg_uT[:, :, t0:t0 + 512].rearrange("g p t -> p g t"), (), (uR[bi],))
            for ci in range(4):
                i = k % 2
                k += 1
                tc0 = t0 + ci * 128
                dma("sp", vf[i], sg_v[tc0:tc0 + 128, :], (), (vR[i],))
                act(junk, vf[i], AF.Square, (vR[i],), (junkR, stR), accum_out=st[:, 0:1])
                act(st[:, 1:2], st[:, 0:1], AF.Sqrt, (stR, cR), (stR,), scale=1.0 / 1024, bias=eps[:, 0:1])
                dve_recip(st[:, 2:3], st[:, 1:2], (stR,), (stR,))
                dve_stt(vn[i], vf[i], st[:, 2:3], sggb, ALU.mult, ALU.mult, (vR[i], stR, cR), (vnR[i],))
                for hb in range(2):
                    b = 2 + (2 * k + hb) % 4
                    pe_group([(PS[b][:, gg * 128:(gg + 1) * 128], vn[i][:, (hb * 4 + gg) * 128:(hb * 4 + gg + 1) * 128],
                               wsb[:, (hb * 4 + gg) * 128:(hb * 4 + gg + 1) * 128], True, True) for gg in range(4)],
                             (vnR[i], cR), (PSR[b],))
                    ti = (2 * k + hb) % 2
                    dve_tt(tmp[ti], PS[b][:, :], bsb[:, hb * 512:(hb + 1) * 512], ALU.add, (PSR[b], cR), (tmpR[ti],))
                    dve_tt(yst[bi][:, hb * 4:hb * 4 + 4, ci * 128:(ci + 1) * 128],
                           tmp[ti][:, :].rearrange("p (g t) -> p g t", g=4),
                           uT[bi][:, hb * 4:hb * 4 + 4, ci * 128:(ci + 1) * 128], ALU.mult,
                           (tmpR[ti], uR[bi]), (yR[bi],), eng="pool")
            dma("pool", ycT[:, :, t0:t0 + 512].rearrange("g p t -> p g t"), yst[bi], (yR[bi],), ())

    def phase_NA(l):
        P.barrier()
        A.reset()
        cR = Res("constNA")
        U = A.f32(8 * 14 * 64)
        load_const("sp", U, Utab[l], cR)
        Uv = U.rearrange("p (h e two q) -> p h e two q", h=8, e=7, two=2)
        flg = A.f32(2)
        load_const("sp", flg, flag_d[:, :], cR)
        ones = A.bf(128)
        memset(ones, 1.0, (cR,))
        SPAN = min(16, min(seqs) // 64)
        NBE, NBO = SPAN // 2, SPAN // 2 - 1
        QT = [A.bf(8 * 512).rearrange("p (h t) -> p h t", h=8) for _ in range(2)]
        QR = [Res("Q0"), Res("Q1")]
        KW = [A.bf(8 * SPAN * 64).rearrange("p (h t) -> p h t", h=8) for _ in range(2)]
        KR = [Res("K0"), Res("K1")]
        Ve = [A.bf(NBE * 1024).rearrange("p (j c) -> p j c", j=NBE) for _ in range(2)]
        VeR = [Res("Ve0"), Res("Ve1")]
        Vo = [A.bf(NBO * 1024).rearrange("p (j c) -> p j c", j=NBO) for _ in range(2)]
        VoR = [Res("Vo0"), Res("Vo1")]
        sb = [A.f32(2048) for _ in range(2)]
        sbR = [[Res("sb%d_%d" % (p_, i)) for i in range(4)] for p_ in range(2)]
        PT = [A.bf(2048) for _ in range(2)]
        PTR = [[Res("PT%d_%d" % (p_, i)) for i in range(4)] for p_ in range(2)]
        rD = [A.f32(512) for _ in range(2)]
        rDR = [Res("rD0"), Res("rD1")]
        yst = [A.bf(8 * 512).rearrange("p (h t) -> p h t", h=8) for _ in range(2)]
        yR = [Res("ya0"), Res("ya1")]
        ytmp = A.bf(8 * 64).rearrange("p (h t) -> p h t", h=8)
        ytR = Res("ytmp")
        rcount = [0]

        def na_row(gi, rl, r, rs, w0, dest, destR):
            o = rs - w0
            base = rs - r + 7
            pp = rcount[0] % 2
            rcount[0] += 1
            Vv, VvR = (Ve[gi], VeR[gi]) if o % 2 == 0 else (Vo[gi], VoR[gi])
            for b in range(4):
                pe_group([(PS[b][:, (hh * 4 + j) * 64:(hh * 4 + j + 1) * 64],
                           KW[gi][:, 2 * b + hh, o * 64 + j * 128:o * 64 + (j + 1) * 128],
                           QT[gi][:, 2 * b + hh, rl * 64:(rl + 1) * 64], True, True)
                          for hh in range(2) for j in range(4)], (KR[gi], QR[gi]), (PSR[b],))
                dve_tt(sb[pp][:, b * 512:(b + 1) * 512].rearrange("p (h j q) -> p h j q", h=2, j=4),
                       PS[b][:, :].rearrange("p (h j q) -> p h j q", h=2, j=4),
                       Uv[:, 2 * b:2 * b + 2, base // 2:base // 2 + 4, base % 2, :], ALU.add,
                       (PSR[b], cR), (sbR[pp][b],))
                act(PT[pp][:, b * 512:(b + 1) * 512], sb[pp][:, b * 512:(b + 1) * 512], AF.Exp, (sbR[pp][b],), (PTR[pp][b],))
            PTv = PT[pp][:, :].rearrange("p (h j q) -> p h j q", h=8, j=4)
            bO, bD = 4 + 2 * pp, 5 + 2 * pp
            pe_group([(PS[bO][:, h * 64:(h + 1) * 64], Vv[:, o // 2 + j, h * 128:(h + 1) * 128],
                       PTv[:, h, j, :], j == 0, j == 3) for h in range(8) for j in range(4)],
                     (VvR,) + tuple(PTR[pp]), (PSR[bO],))
            pe_group([(PS[bD][:, :], ones[:, :], PTv[:, :, j, :], j == 0, j == 3) for j in range(4)],
                     (cR,) + tuple(PTR[pp]), (PSR[bD],))
            act(rD[pp], PS[bD][:, :], AF.Ln, (PSR[bD],), (rDR[pp],))
            act(rD[pp], rD[pp], AF.Exp, (rDR[pp],), (rDR[pp],), scale=-1.0)
            dve_tt(dest, PS[bO][:, :].rearrange("p (h q) -> p h q", h=8),
                   rD[pp][:, :].rearrange("p (h q) -> p h q", h=8), ALU.mult, (PSR[bO], rDR[pp]), (destR,))

        gcount = 0
        s0 = 0
        if l + 1 < depth:
            bg_casts.extend(cast_list(l + 1))
        ngroups = sum(sl_ // 512 for sl_ in seqs)
        ncast_per_group = (len(bg_casts) + ngroups - 1) // ngroups
        for sl in seqs:
            rows = sl // 64
            span = min(16, rows)
            assert span == SPAN
            rb = None if vsplit is None else vsplit // 64
            for rg in range(rows // 8):
                gi = gcount % 2
                gcount += 1
                r0 = rg * 8
                w0 = min(max(r0 - 4, 0), max(rows - 16, 0))
                tq0 = s0 + rg * 512
                tw0 = s0 + w0 * 64
                dma("sp", QT[gi], na_qT[:, :, tq0:tq0 + 512].rearrange("h p t -> p h t"), (), (QR[gi],))
                dma("sp", KW[gi], na_kT[:, :, tw0:tw0 + span * 64].rearrange("h p t -> p h t"), (), (KR[gi],))
                dma("sp", Ve[gi], na_v[tw0:tw0 + span * 64, :].rearrange("(j p) c -> p j c", p=128), (), (VeR[gi],))
                dma("sp", Vo[gi], na_v[tw0 + 64:tw0 + 64 + NBO * 128, :].rearrange("(j p) c -> p j c", p=128), (), (VoR[gi],))
                for rl in range(8):
                    r = r0 + rl
                    rsA = min(max(r - 4, 0), rows - 8)
                    dest = yst[gi][:, :, rl * 64:(rl + 1) * 64]
                    na_row(gi, rl, r, rsA, w0, dest, yR[gi])
                    if rb is not None:
                        rsB = min(max(r - 4, 0), rb - 8) if r < rb else rb + min(max(r - rb - 4, 0), rows - rb - 8)
                        if rsB != rsA:
                            na_row(gi, rl, r, rsB, w0, ytmp, ytR)
                            dve_ts(dest, dest, flg[:, 0:1], None, ALU.mult, None, (yR[gi], cR), (yR[gi],))
                            dve_stt(dest, ytmp, flg[:, 1:2], dest, ALU.mult, ALU.add, (ytR, cR, yR[gi]), (yR[gi],))
                dma("pool", yaT[:, :, tq0:tq0 + 512].rearrange("h p t -> p h t"), yst[gi], (yR[gi],), ())
                for _ in range(min(len(bg_casts), ncast_per_group)):
                    bg_casts.pop(0)()
            s0 += sl
        while bg_casts:
            bg_casts.pop(0)()

    def phase_ML(l, bwd):
        P.barrier()
        A.reset()
        cR = Res("constML")
        ident = A.f32(128)
        load_const("sp", ident, ident_d[:, :], cR)
        identb = A.bf(128)
        P.op("dve", lambda h: h.tensor_copy(out=identb, in_=ident), (cR,), (cR,))
        tri = A.f32(128)
        load_const("sp", tri, (trib_d if bwd else trif_d)[:, :], cR)
        onesf = A.f32(128)
        memset(onesf, 1.0, (cR,))
        eps = A.f32(1)
        memset(eps, EPS, (cR,))
        flg = A.f32(2)
        load_const("sp", flg, flag_d[:, :], cR)
        mlgb = None
        if bwd:
            mlgb = A.f32(1024)
            load_const("sp", mlgb, mlg[l].partition_broadcast(128), cR)
        def gtile():
            return A.f32(512)
        It, Ft, sp_, cs, csd, gg, tmpg, ek, ea, rmask = [gtile() for _ in range(10)]
        gR = Res("gates")
        small = A.f32(64)
        smallT = A.f32(4 * 64)
        Dm = A.f32(4 * 64)
        ekT = A.f32(4 * 64)
        eaT = A.f32(4 * 64)
        eibc = A.f32(4 * 64)
        gtR = Res("gT")
        C = A.f32(8 * 257).rearrange("p (c e) -> p c e", c=8)
        CR = [Res("C%d" % h) for h in range(4)]
        Db = A.bf(8 * 258).rearrange("p (c e) -> p c e", c=8)
        DR = [Res("D%d" % h) for h in range(4)]
        qT = [A.bf(8 * 512).rearrange("p (c t) -> p c t", c=8) for _ in range(2)]
        qR = [Res("q0"), Res("q1")]
        kT = [A.bf(8 * 512).rearrange("p (c t) -> p c t", c=8) for _ in range(2)]
        kR = [Res("k0"), Res("k1")]
        va = [A.bf(4 * 4 * 258).rearrange("p (j h e) -> p j h e", j=4, h=4) for _ in range(2)]
        vaR = [Res("va0"), Res("va1")]
        kp = A.bf(1024).rearrange("p (h d) -> p h d", h=4)
        kpR = Res("kp")
        sT = [A.bf(128), A.bf(128)]
        sTR = [Res("sT0"), Res("sT1")]
        den = A.f32(16)
        denR2 = [Res("den0"), Res("den1")]
        hst = [A.f32(1024), A.f32(1024)]
        hstR = [Res("hst0"), Res("hst1")]
        if bwd:
            hf = [A.f32(4 * 1024).rearrange("p (j c) -> p j c", j=4) for _ in range(2)]
            hfR = [Res("hf0"), Res("hf1")]
            ob = [A.f32(4 * 1024).rearrange("p (j c) -> p j c", j=4) for _ in range(2)]
            obR = [Res("o0"), Res("o1")]
            gsig = A.f32(1024)
            gsigR = Res("gsig")
            hsum = [A.f32(256), A.f32(256)]
            hsumR = [Res("hsum0"), Res("hsum1")]
            junk = A.bf(256)
            yb = [A.bf(256), A.bf(256)]
            ybR = [Res("yb0"), Res("yb1")]
            ybst = [A.bf(8 * 512).rearrange("p (c t) -> p c t", c=8) for _ in range(2)]
            ybstR = [Res("ybst0"), Res("ybst1")]
        for h4 in range(2):
            memset(va[h4][:, :, :, 256:257], 1.0, (vaR[h4],))
        memset(rmask, 1.0, (gR,))
        memset(rmask[:, :].rearrange("p (h j) -> p h j", h=4)[:, :, 0:1], 0.0, (gR,))

        def seq_body(s0, sl):
            nch = sl // 128
            assert nch <= 64
            r0 = 8 if bwd else 0
            np_ = nch
            Iv, Fv = It[0:np_, :], Ft[0:np_, :]
            dma("sp", Iv.rearrange("p (h j) -> p h j", h=4), ml_g[r0:r0 + 4, s0:s0 + sl].rearrange("h (c j) -> c h j", j=128), (), (gR,))
            dma("sp", Fv.rearrange("p (h j) -> p h j", h=4), ml_g[r0 + 4:r0 + 8, s0:s0 + sl].rearrange("h (c j) -> c h j", j=128), (), (gR,))
            spv, csv, csdv, ggv, tmv, ekv, eav, rmv = [t[0:np_, :] for t in (sp_, cs, csd, gg, tmpg, ek, ea, rmask)]
            v3 = lambda t: t.rearrange("p (h j) -> p h j", h=4)
            act(spv, Fv, AF.Exp, (gR,), (gR,), scale=-1.0)
            act(spv, spv, AF.Ln, (gR,), (gR,), bias=1.0)
            P.op("dve", lambda h: h.tensor_tensor_scan(out=csv, data0=rmv, data1=spv, initial=0.0, op0=ALU.mult, op1=ALU.add), (gR,), (gR,))
            tot = small[0:np_, 0:4]
            Gmax = small[0:np_, 4:8]
            Mv = small[0:np_, 8:12]
            mpv = small[0:np_, 12:16]
            eiv = small[0:np_, 16:20]
            P.op("dve", lambda h: h.tensor_copy(out=tot, in_=v3(csv)[:, :, 127]), (gR,), (gR,))
            if not bwd:
                P.op("dve", lambda h: h.tensor_copy(out=csdv, in_=csv), (gR,), (gR,))
            else:
                dve_tt(tmv, spv, csv, ALU.subtract, (gR,), (gR,))
                for h_ in range(4):
                    dve_ts(v3(csdv)[:, h_, :], v3(tmv)[:, h_, :], tot[:, h_:h_ + 1], None, ALU.add, None, (gR,), (gR,))
            dve_tt(ggv, Iv, csdv, ALU.add, (gR,), (gR,))
            P.op("dve", lambda h: h.tensor_reduce(out=Gmax, in_=v3(ggv), axis=AX.X, op=ALU.max), (gR,), (gR,))
            GmT = smallT[0:4, 0:64]
            toT = smallT[0:4, 64:128]
            MT = smallT[0:4, 128:192]
            mpT = smallT[0:4, 192:256]
            pe_tr([(PS[0][0:4, 0:np_], Gmax), (PS[0][0:4, 64:64 + np_], tot)], ident[0:np_, 0:np_], (gR, cR), (PSR[0],))
            P.op("dve", lambda h: h.tensor_copy(out=smallT[0:4, 0:128], in_=PS[0][0:4, 0:128]), (PSR[0],), (gtR,))
            order = list(range(nch - 1, -1, -1)) if bwd else list(range(nch))
            memset(mpT[:, order[0]:order[0] + 1], 0.0, (gtR,))
            for oi, c in enumerate(order):
                dve_tt(MT[:, c:c + 1], GmT[:, c:c + 1], mpT[:, c:c + 1], ALU.max, (gtR,), (gtR,))
                if oi + 1 < nch:
                    cn = order[oi + 1]
                    dve_tt(mpT[:, cn:cn + 1], MT[:, c:c + 1], toT[:, c:c + 1], ALU.subtract, (gtR,), (gtR,))
                    if vsplit is not None and max(c, cn) * 128 == vsplit:
                        dve_ts(mpT[:, cn:cn + 1], mpT[:, cn:cn + 1], flg[0:4, 0:1], None, ALU.mult, None, (gtR, cR), (gtR,))
            pe_tr([(PS[0][0:np_, 128:132], MT[:, 0:np_]), (PS[0][0:np_, 132:136], mpT[:, 0:np_])], ident[0:4, 0:4], (gtR, cR), (PSR[0],))
            P.op("dve", lambda h: h.tensor_copy(out=small[0:np_, 8:16], in_=PS[0][0:np_, 128:136]), (PSR[0],), (gR,))
            for h_ in range(4):
                dve_ts(v3(tmv)[:, h_, :], v3(ggv)[:, h_, :], Mv[:, h_:h_ + 1], None, ALU.subtract, None, (gR,), (gR,))
            act(ekv, tmv, AF.Exp, (gR,), (gR,))
            for h_ in range(4):
                dve_ts(v3(tmv)[:, h_, :], v3(csdv)[:, h_, :], Mv[:, h_:h_ + 1], None, ALU.subtract, None, (gR,), (gR,))
            act(eav, tmv, AF.Exp, (gR,), (gR,))
            dve_tt(eiv, mpv, Mv, ALU.subtract, (gR,), (gR,))
            act(eiv, eiv, AF.Exp, (gR,), (gR,))
            pe_tr([(PS[1][:, h_ * 64:h_ * 64 + np_], v3(ekv)[:, h_, :]) for h_ in range(4)], ident[0:np_, 0:np_], (gR, cR), (PSR[1],))
            pe_tr([(PS[1][:, 256 + h_ * 64:256 + h_ * 64 + np_], v3(eav)[:, h_, :]) for h_ in range(4)], ident[0:np_, 0:np_], (gR, cR), (PSR[1],))
            for h_ in range(4):
                P.op("dve", lambda h, h_=h_: h.tensor_copy(out=ekT[:, h_ * 64:h_ * 64 + np_], in_=PS[1][:, h_ * 64:h_ * 64 + np_]), (PSR[1],), (gtR,))
                P.op("dve", lambda h, h_=h_: h.tensor_copy(out=eaT[:, h_ * 64:h_ * 64 + np_], in_=PS[1][:, 256 + h_ * 64:256 + h_ * 64 + np_]), (PSR[1],), (gtR,))
            for h_ in range(4):
                dve_ts(Dm[0:np_, h_ * 64:h_ * 64 + np_], ident[0:np_, 0:np_], eiv[:, h_:h_ + 1], None, ALU.mult, None, (gR, cR), (gtR,))
            pe_group([(PS[0][:, 256 + h_ * 64:256 + h_ * 64 + np_], onesf[0:np_, :], Dm[0:np_, h_ * 64:h_ * 64 + np_], True, True) for h_ in range(4)],
                     (gtR, cR), (PSR[0],))
            for h_ in range(4):
                P.op("dve", lambda h, h_=h_: h.tensor_copy(out=eibc[:, h_ * 64:h_ * 64 + np_], in_=PS[0][:, 256 + h_ * 64:256 + h_ * 64 + np_]), (PSR[0],), (gtR,))

            if vsplit is not None:
                cb = vsplit // 128 - (1 if bwd else 0)
                ev = eibc[:, :].rearrange("p (h c) -> p h c", h=4)[:, :, cb:cb + 1]
                dve_ts(ev, ev, flg[:, 0:1], None, ALU.mult, None, (gtR, cR), (gtR,))
            for h_ in range(4):
                memset(C[:, 2 * h_:2 * h_ + 2, :], 0.0, (CR[h_],))
            nsc = sl // 512
            sc_order = list(range(nsc - 1, -1, -1)) if bwd else list(range(nsc))
            for sci, sc in enumerate(sc_order):
                bi = sci % 2
                ts0 = s0 + sc * 512
                dma("sp", qT[bi], ml_qT[:, :, ts0:ts0 + 512].rearrange("c p t -> p c t"), (), (qR[bi],))
                dma("sp", kT[bi], ml_kT[:, :, ts0:ts0 + 512].rearrange("c p t -> p c t"), (), (kR[bi],))
                for j in range(4):
                    dma("sp", va[bi][:, j, :, 0:256], ml_v[ts0 + j * 128:ts0 + (j + 1) * 128, :].rearrange("p (h e) -> p h e", h=4), (), (vaR[bi],))
                if bwd:
                    dma("sp", hf[bi], hfwd[ts0:ts0 + 512, :].rearrange("(j p) c -> p j c", p=128), (), (hfR[bi],))
                    dma("sp", ob[bi], ml_o[ts0:ts0 + 512, :].rearrange("(j p) c -> p j c", p=128), (), (obR[bi],))
                ci_order = [3, 2, 1, 0] if bwd else [0, 1, 2, 3]
                for ci in ci_order:
                    c = sc * 4 + ci
                    cols = slice(ci * 128, (ci + 1) * 128)
                    psKb = PS[7][:, :].bitcast(BF16)
                    pe_tr([(psKb[:, cc * 128:(cc + 1) * 128], kT[bi][:, cc, cols]) for cc in range(8)], identb, (kR[bi], cR), (PSR[7],))
                    for h_ in range(4):
                        src = psKb[:, h_ * 256:(h_ + 1) * 256]
                        if h_ % 2 == 0:
                            act(kp[:, h_, :], src, AF.Copy, (PSR[7], gtR), (kpR,), scale=ekT[:, h_ * 64 + c:h_ * 64 + c + 1])
                        else:
                            dve_ts(kp[:, h_, :], src, ekT[:, h_ * 64 + c:h_ * 64 + c + 1], None, ALU.mult, None, (PSR[7], gtR), (kpR,))
                    if bwd:
                        act(gsig, ob[bi][:, ci, :], AF.Sigmoid, (obR[bi],), (gsigR,))
                        dve_tt(gsig, gsig, mlgb, ALU.mult, (gsigR, cR), (gsigR,), eng="pool")
                    hi = (sci * 4 + ci) % 2
                    for pr_ in range(2):
                        hs = (2 * pr_, 2 * pr_ + 1)

                        def bk(h_):
                            st_ = h_ % 2
                            return st_, 1 + 3 * st_, 2 + 3 * st_, 3 + 3 * st_

                        def sc_(tab, h_):
                            return tab[:, h_ * 64 + c:h_ * 64 + c + 1]

                        for h_ in hs:
                            act(Db[:, 2 * h_:2 * h_ + 2, 0:257], C[:, 2 * h_:2 * h_ + 2, :], AF.Copy, (CR[h_], gtR), (DR[h_],), scale=sc_(eibc, h_))
                        for h_ in hs:
                            st_, bS, bN, bC = bk(h_)
                            pe_group([(PS[bS][:, 0:128], kT[bi][:, 2 * h_ + dc, cols], qT[bi][:, 2 * h_ + dc, cols], dc == 0, dc == 1) for dc in range(2)],
                                     (kR[bi], qR[bi]), (PSR[bS],))
                        for h_ in hs:
                            st_, bS, bN, bC = bk(h_)
                            dve_stt(sT[st_], PS[bS][:, 0:128], sc_(ekT, h_), tri, ALU.mult, ALU.mult, (PSR[bS], gtR, cR), (sTR[st_],))
                        for h_ in hs:
                            st_, bS, bN, bC = bk(h_)
                            pe_group([(PS[bC][:, dc * 256:(dc + 1) * 256], kp[:, h_, dc * 128:(dc + 1) * 128], va[bi][:, ci, h_, 0:256], True, True) for dc in range(2)],
                                     (kpR, vaR[bi]), (PSR[bC],))
                        for h_ in hs:
                            st_, bS, bN, bC = bk(h_)
                            pe_group([(PS[bN][:, 0:257], sT[st_], va[bi][:, ci, h_, 0:257], True, False)] +
                                     [(PS[bN][:, 0:257], qT[bi][:, 2 * h_ + dc, cols], Db[:, 2 * h_ + dc, 0:257], False, dc == 1) for dc in range(2)],
                                     (sTR[st_], vaR[bi], qR[bi], DR[h_]), (PSR[bN],))
                            pe_group([(PS[bN][:, 300 + dc:301 + dc], kp[:, h_, dc * 128:(dc + 1) * 128], va[bi][:, ci, h_, 256:257], True, True) for dc in range(2)],
                                     (kpR, vaR[bi]), (PSR[bN],))
                        for h_ in hs:
                            st_, bS, bN, bC = bk(h_)
                            dn = den[:, 8 * st_:8 * st_ + 8]
                            dR_ = denR2[st_]
                            act(dn[:, 5:6], PS[bN][:, 256:257], AF.Abs, (PSR[bN],), (dR_,))
                            dve_ts(dn[:, 0:1], dn[:, 5:6], sc_(eaT, h_), None, ALU.max, None, (dR_, gtR), (dR_,))
                            dve_recip(dn[:, 1:2], dn[:, 0:1], (dR_,), (dR_,))
                            if not bwd:
                                dve_ts(hst[hi][:, h_ * 256:(h_ + 1) * 256], PS[bN][:, 0:256], dn[:, 1:2], None, ALU.mult, None, (PSR[bN], dR_), (hstR[hi],))
                            else:
                                dve_stt(hsum[st_], PS[bN][:, 0:256], dn[:, 1:2], hf[bi][:, ci, h_ * 256:(h_ + 1) * 256], ALU.mult, ALU.add,
                                        (PSR[bN], dR_, hfR[bi]), (hsumR[st_],))
                                act(junk, hsum[st_], AF.Square, (hsumR[st_],), (dR_,), accum_out=dn[:, 2:3])
                                act(dn[:, 3:4], dn[:, 2:3], AF.Sqrt, (dR_, cR), (dR_,), scale=1.0 / 256, bias=eps[:, 0:1])
                                dve_recip(dn[:, 4:5], dn[:, 3:4], (dR_,), (dR_,))
                                dve_stt(yb[st_], hsum[st_], dn[:, 4:5], gsig[:, h_ * 256:(h_ + 1) * 256], ALU.mult, ALU.mult, (hsumR[st_], dR_, gsigR), (ybR[st_],))
                        if bwd:
                            psYb = PS[0][:, :].bitcast(BF16)
                            for h_ in hs:
                                st_ = h_ % 2
                                pe_tr([(psYb[:, (2 * st_ + dc) * 128:(2 * st_ + dc + 1) * 128], yb[st_][:, dc * 128:(dc + 1) * 128]) for dc in range(2)],
                                      identb, (ybR[st_], cR), (PSR[0],))
                            evac_copy(ybst[bi][:, 4 * pr_:4 * pr_ + 4, cols], psYb[:, 0:512].rearrange("p (a b) -> p a b", a=4), (PSR[0],), (ybstR[bi],))
                        for h_ in hs:
                            st_, bS, bN, bC = bk(h_)
                            dve_stt(C[:, 2 * h_:2 * h_ + 2, 0:256], C[:, 2 * h_:2 * h_ + 2, 0:256], sc_(eibc, h_),
                                    PS[bC][:, :].rearrange("p (a b) -> p a b", a=2), ALU.mult, ALU.add, (CR[h_], gtR, PSR[bC]), (CR[h_],))
                            dve_stt(C[:, 2 * h_:2 * h_ + 2, 256:257], C[:, 2 * h_:2 * h_ + 2, 256:257], sc_(eibc, h_),
                                    PS[bN][:, 300:302].rearrange("p (a b) -> p a b", a=2), ALU.mult, ALU.add, (CR[h_], gtR, PSR[bN]), (CR[h_],))
                    if not bwd:
                        tcs = ts0 + ci * 128
                        dma("pool", hfwd[tcs:tcs + 128, :], hst[hi], (hstR[hi],), ())
                if bwd:
                    dma("pool", ybT[:, :, ts0:ts0 + 512].rearrange("c p t -> p c t"), ybst[bi], (ybstR[bi],), ())

        s0 = 0
        for sl in seqs:
            seq_body(s0, sl)
            s0 += sl

    def phase_C(l, xsrc, xdst):
        P.barrier()
        A.reset()
        cR = Res("constC")
        ident = A.f32(128)
        load_const("sp", ident, ident_d[:, :], cR)
        onesf = A.f32(128)
        memset(onesf, 1.0, (cR,))
        gv = [A.f32(16) for _ in range(4)]
        for i in range(4):
            load_const("sp", gv[i], gvec[l, i], cR)
        eps = A.f32(1)
        memset(eps, EPS, (cR,))
        xT = A.f32(16 * NTC).rearrange("p (k n) -> p k n", k=16)
        xTR = Res("xT")
        hT = A.bf(16 * NTC).rearrange("p (k n) -> p k n", k=16)
        hR = Res("hT")
        oT = A.f32(16 * NTC).rearrange("p (k n) -> p k n", k=16)
        oTR = Res("oT")
        wg = [A.bf(16 * 512).rearrange("p (k n) -> p k n", k=16) for _ in range(2)]
        wgR = [Res("wg0"), Res("wg1")]
        wbv = [A.bf(8 * 512).rearrange("p (k n) -> p k n", k=8) for _ in range(2)]
        wbR = [Res("wb0"), Res("wb1")]
        rs_bc = A.f32(NTC)
        rsR = Res("rs")
        NSQ = 4
        sq = [A.f32(NTC) for _ in range(NSQ)]
        sqR = [Res("sq%d" % i) for i in range(NSQ)]
        r1_mark = A.off
        yT = A.bf(8 * NTC).rearrange("p (k n) -> p k n", k=8)
        yTR = Res("yT")
        gs = [A.f32(NTC) for _ in range(2)]
        gsR = [Res("gs0"), Res("gs1")]
        mT = A.bf(16 * NTC).rearrange("p (k n) -> p k n", k=16)
        mTR = Res("mT")
        end1 = A.off
        A.off = r1_mark
        fT = A.bf(64 * NTC).rearrange("p (k n) -> p k n", k=64)
        fTR = Res("fT")
        end2 = A.off
        A.off = r1_mark
        nctx = make_normctx(ident)
        nctx.eps = eps
        xo = [A.f32(2048) for _ in range(2)]
        xoR = [Res("xo0"), Res("xo1")]
        end3 = A.off
        A.off = max(end1, end2, end3)
        winl = win_bf[l].rearrange("(k p) c -> p k c", p=128)
        branches = [(yaT, wa_bf), (ybT, wb_bf), (ycT, wc_bf)]
        woutl = wout_bf[l].rearrange("(k p) c -> p k c", p=128)
        wupl = wup_bf[l].rearrange("(k p) c -> p k c", p=128)
        wdl = wdown_bf[l].rearrange("(k p) c -> p k c", p=128)
        cnt = {"wg": 0, "wb": 0, "b": 0, "sq": 0, "gs": 0}

        def nb():
            b = 2 + cnt["b"] % 4
            cnt["b"] += 1
            return b

        pend = {}

        def ldwg(src, key=None):
            if key is not None and key in pend:
                return pend.pop(key)
            i = cnt["wg"] % 2
            cnt["wg"] += 1
            dma("sp", wg[i], src, (), (wgR[i],))
            return wg[i], wgR[i]

        def ldwb(src, key=None):
            if key is not None and key in pend:
                return pend.pop(key)
            i2 = cnt["wb"] % 2
            cnt["wb"] += 1
            dma("sp", wbv[i2], src, (), (wbR[i2],))
            return wbv[i2], wbR[i2]

        stq = []

        def stats_push(srcT, srcR, fb, flush=False):
            if fb is not None:
                i = cnt["sq"] % NSQ
                cnt["sq"] += 1
                act(sq[i], srcT[:, fb, :], AF.Square, (srcR,), (sqR[i],))
                stq.append((i, fb))
            while stq and (flush or len(stq) > 2):
                i, f_ = stq.pop(0)
                pe_group([(PS[6][:, :], onesf[:, :], sq[i], f_ == 0, f_ == 15)], (sqR[i], cR), (PSR[6],))

        def rstd_from_stats():
            act(rs_bc, PS[6][:, :], AF.Ln, (PSR[6], cR), (rsR,), scale=1.0 / D, bias=eps[:, 0:1])
            act(rs_bc, rs_bc, AF.Exp, (rsR,), (rsR,), scale=-0.5)

        def stats_and_rstd(srcT, srcR):
            for fb in range(16):
                stats_push(srcT, srcR, fb)
            stats_push(None, None, None, flush=True)
            rstd_from_stats()

        def residual_update(gcol):
            for fb in range(16):
                i = cnt["sq"] % NSQ
                cnt["sq"] += 1
                dve_tt(sq[i], oT[:, fb, :], rs_bc, ALU.mult, (oTR, rsR), (sqR[i],), eng="pool")
                dve_stt(xT[:, fb, :], sq[i], gcol[:, fb:fb + 1], xT[:, fb, :], ALU.mult, ALU.add, (sqR[i], cR, xTR), (xTR,))

        G = Res("G")
        P.ambient = (G,)
        for t0 in range(0, T, NTC):
            wbl0 = branches[0][1][l].rearrange("(k p) c -> p k c", p=128)
            pend["g0"] = ldwg(winl[:, :, OFF_G:OFF_G + 512])
            pend["b0"] = ldwb(wbl0[:, :, 0:512])
            for tt in range(NTC // 128):
                norm_tile(nctx, xsrc[t0 + tt * 128:t0 + (tt + 1) * 128, :], gv[0], cR, hT, hR, tt * 128,
                          rawT=xT, rawR=xTR, constR=(cR,))
            P.fence(("sp", "act", "dve", "pool"), G)
            for bi, (ysrc, wsrc) in enumerate(branches):
                dma("sp", yT, ysrc[:, :, t0:t0 + NTC].rearrange("c p t -> p c t"), (), (yTR,))
                wbl = wsrc[l].rearrange("(k p) c -> p k c", p=128)
                for pi in range(4):
                    first = (bi == 0 and pi == 0)
                    wv, wr = ldwg(winl[:, :, OFF_G + bi * 2048 + pi * 512:OFF_G + bi * 2048 + (pi + 1) * 512], "g0" if first else None)
                    wbv_, wbr_ = ldwb(wbl[:, :, pi * 512:(pi + 1) * 512], "b0" if first else None)
                    for j in range(4):
                        fb = pi * 4 + j
                        b1 = nb()
                        pe_group([(PS[b1][:, :], wv[:, kc, j * 128:(j + 1) * 128], hT[:, kc, :], kc == 0, kc == 15) for kc in range(16)],
                                 (wr, hR), (PSR[b1],))
                        gi = cnt["gs"] % 2
                        cnt["gs"] += 1
                        act(gs[gi], PS[b1][:, :], AF.Sigmoid, (PSR[b1],), (gsR[gi],))
                        b2 = nb()
                        pe_group([(PS[b2][:, :], wbv_[:, kc, j * 128:(j + 1) * 128], yT[:, kc, :], kc == 0, kc == 7) for kc in range(8)],
                                 (wbr_, yTR), (PSR[b2],))
                        if bi == 0:
                            dve_tt(oT[:, fb, :], PS[b2][:, :], gs[gi], ALU.mult, (PSR[b2], gsR[gi]), (oTR,))
                        else:
                            dve_tt(gs[gi], PS[b2][:, :], gs[gi], ALU.mult, (PSR[b2], gsR[gi]), (gsR[gi],))
                            if bi == 1:
                                dve_tt(oT[:, fb, :], oT[:, fb, :], gs[gi], ALU.add, (oTR, gsR[gi]), (oTR,), eng="pool")
                            else:
                                dve_tt(mT[:, fb, :], oT[:, fb, :], gs[gi], ALU.add, (oTR, gsR[gi]), (mTR,), eng="pool")
            for pi in range(4):
                wv, wr = ldwg(woutl[:, :, pi * 512:(pi + 1) * 512])
                for j in range(4):
                    fb = pi * 4 + j
                    b1 = nb()
                    pe_group([(PS[b1][:, :], wv[:, kc, j * 128:(j + 1) * 128], mT[:, kc, :], kc == 0, kc == 15) for kc in range(16)],
                             (wr, mTR), (PSR[b1],))
                    evac_copy(oT[:, fb, :], PS[b1][:, :], (PSR[b1],), (oTR,))
                    stats_push(oT, oTR, fb)
            stats_push(None, None, None, flush=True)
            rstd_from_stats()
            residual_update(gv[1])
            stats_and_rstd(xT, xTR)
            for fb in range(16):
                i = cnt["sq"] % NSQ
                cnt["sq"] += 1
                dve_tt(sq[i], xT[:, fb, :], rs_bc, ALU.mult, (xTR, rsR), (sqR[i],))
                act(hT[:, fb, :], sq[i], AF.Copy, (sqR[i], cR), (hR,), scale=gv[2][:, fb:fb + 1])
            P.fence(("dve",), G)
            for pi in range(16):
                wv, wr = ldwg(wupl[:, :, pi * 512:(pi + 1) * 512])
                for j in range(4):
                    fb = pi * 4 + j
                    b1 = nb()
                    pe_group([(PS[b1][:, :], wv[:, kc, j * 128:(j + 1) * 128], hT[:, kc, :], kc == 0, kc == 15) for kc in range(16)],
                             (wr, hR), (PSR[b1],))
                    i = cnt["sq"] % NSQ
                    cnt["sq"] += 1
                    act(sq[i], PS[b1][:, :], AF.Relu, (PSR[b1],), (sqR[i],))
                    dve_tt(fT[:, fb, :], sq[i], sq[i], ALU.mult, (sqR[i],), (fTR,))
            for cp in range(4):
                for kq in range(4):
                    wv, wr = ldwg(wdl[:, kq * 16:(kq + 1) * 16, cp * 512:(cp + 1) * 512])
                    for j in range(4):
                        pe_group([(PS[2 + j][:, :], wv[:, kc, j * 128:(j + 1) * 128], fT[:, kq * 16 + kc, :],
                                   kq == 0 and kc == 0, kq == 3 and kc == 15) for kc in range(16)],
                                 (wr, fTR), (PSR[2 + j],))
                for j in range(4):
                    evac_copy(oT[:, cp * 4 + j, :], PS[2 + j][:, :], (PSR[2 + j],), (oTR,))
                    stats_push(oT, oTR, cp * 4 + j)
            stats_push(None, None, None, flush=True)
            rstd_from_stats()
            residual_update(gv[3])
            P.fence(("act", "dve", "sp"), G)
            for tt in range(NTC // 128):
                oi = tt % 2
                for b in range(4):
                    pb = trbank()
                    pe_tr([(PS[pb][:, j * 128:(j + 1) * 128], xT[:, 4 * b + j, tt * 128:(tt + 1) * 128]) for j in range(4)],
                          ident, (xTR, cR), (PSR[pb],))
                    evac_copy(xo[oi][:, b * 512:(b + 1) * 512], PS[pb][:, :], (PSR[pb],), (xoR[oi],))
                dma("pool", xdst[t0 + tt * 128:t0 + (tt + 1) * 128, :], xo[oi], (xoR[oi],), ())
        P.ambient = ()
        assert not pend

    phase_cast()
    for l in range(depth):
        xsrc = x_in if l == 0 else xmid
        xdst = y_out if l == depth - 1 else xmid
        phase_A(l, xsrc)
        phase_SG(l)
        phase_NA(l)
        phase_ML(l, False)
        phase_ML(l, True)
        phase_C(l, xsrc, xdst)
    P.barrier()
    P.emit(nc, stack)
    stack.close()
    return nc


def _na_table(rpb):
    H = rpb.shape[0]
    pad = np.concatenate([rpb, np.full((H, 15, 1), NEG, np.float32)], axis=2)
    kcol = np.arange(64)[:, None]
    q = np.arange(64)[None, :]
    cs = np.clip(q - 8, 0, 48)
    ok = (kcol >= cs) & (kcol < cs + 16)
    dc = np.where(ok, np.clip(kcol - q + 15, 0, 30), 31)
    U = np.empty((2, 64, H, 14, 64), np.float32)
    for a in range(2):
        for e in range(14):
            U[a, :, :, e, :] = pad[:, e + a][:, dc].transpose(1, 0, 2)
    return np.ascontiguousarray(U.reshape(128, H * 14 * 64))


def _prep_common(inp, depth):
    f = lambda a: np.ascontiguousarray(np.asarray(a, dtype=np.float32))
    c = {}
    for k in ("w_in", "w_a", "w_b", "w_c", "w_out", "w_up", "w_down"):
        c[k] = f(inp[k])
    gs = np.stack([f(inp[k]) for k in ("pre_mix_g", "post_mix_g", "pre_mlp_g", "post_mlp_g")], axis=1)
    c["gvec"] = np.ascontiguousarray(gs.reshape(depth, 4, 16, 128).transpose(0, 1, 3, 2))
    c["bgate"] = f(inp["b_gate"]).reshape(depth, 16, 1)
    cw = f(inp["conv_w"])
    c["convw"] = np.ascontiguousarray(cw.reshape(depth, 3, 16, 128).transpose(0, 3, 2, 1).reshape(depth, 128, 48))
    c["mlg"] = f(inp["ml_norm_g"]).reshape(depth, 1, 1024)
    c["sgg"] = f(inp["sg_norm_g"]).reshape(depth, 1, 1024)
    ws = f(inp["w_s"])
    c["wsT"] = np.ascontiguousarray(ws.transpose(0, 3, 1, 2).reshape(depth, 128, 1024))
    c["bsv"] = f(inp["b_s"]).reshape(depth, 1, 1024)
    c["Utab"] = np.stack([_na_table(f(inp["na_rpb"])[l]) for l in range(depth)])
    c["ident"] = np.eye(128, dtype=np.float32)
    j = np.arange(128)[:, None]
    i = np.arange(128)[None, :]
    c["trif"] = (j <= i).astype(np.float32)
    c["trib"] = (j >= i).astype(np.float32)
    return c


_CACHE = {}


def kernel(**inputs):
    xp = np.asarray(inputs["x_prompt"], dtype=np.float32)
    xs = np.asarray(inputs["x_sample"], dtype=np.float32)
    depth = inputs["w_in"].shape[0]
    B, S, _ = xp.shape
    DB, DS, _ = xs.shape
    assert DS == 2 * S and B == 8 and DB == 2
    key = (S, DS, depth)
    if key not in _CACHE:
        _CACHE[key] = build([DS], depth, vsplit=S)
    nc = _CACHE[key]
    common = _prep_common(inputs, depth)
    in_maps = []
    for c in range(8):
        if c < 4:
            x = np.concatenate([xp[2 * c], xp[2 * c + 1]], axis=0)
            fl = 0.0
        elif c < 6:
            x = xs[c - 4]
            fl = 1.0
        else:
            x = np.zeros((DS, D), np.float32)
            fl = 1.0
        m = dict(common)
        m["x"] = np.ascontiguousarray(x)
        m["flag"] = np.tile(np.array([[fl, 1.0 - fl]], np.float32), (128, 1))
        in_maps.append(m)
    res = run_bass_kernel_spmd(nc, in_maps, core_ids=list(range(8)))
    yp = np.empty((B, S, D), np.float32)
    for c in range(4):
        y = res.results[c]["y"]
        yp[2 * c] = y[:S]
        yp[2 * c + 1] = y[S:]
    ys = np.stack([res.results[4 + i]["y"] for i in range(DB)]).astype(np.float32)
    return (yp, ys)
```
